# Optimizing a Trainium2 kernel written in Bass

```python
import jax, jax.numpy as jnp
from jax import lax
import numpy as np

D_MODEL = 1024
BATCH = 16
SEQ = 2048
DEPTH = 2

MEM_LEN = 256
C_CONV = D_MODEL // 2
CONV_K = 31
C_POOL = D_MODEL // 2
POOL_WINDOWS = (2, 4, 8, 16)
N_POOL_GROUPS = len(POOL_WINDOWS)
POOL_GROUP_DIM = C_POOL // N_POOL_GROUPS
POOL_GROUP_OUT = D_MODEL // N_POOL_GROUPS
N_IN = 2 * C_CONV + C_POOL + 2 * D_MODEL
XA_HEADS = 4
XA_HEAD_DIM = D_MODEL // XA_HEADS
D_FF = 2816
FFN_CONV_K = 3
EPS = 1e-6

kernel_name = "hybrid_conformer_pool_gated_block"


def rms_norm(x, g):
    xf = x.astype(jnp.float32)
    y = xf * lax.rsqrt(jnp.mean(xf * xf, axis=-1, keepdims=True) + EPS)
    return (y * g.astype(jnp.float32)).astype(x.dtype)


def layer_norm(x, g, b):
    xf = x.astype(jnp.float32)
    mu = jnp.mean(xf, axis=-1, keepdims=True)
    xc = xf - mu
    var = jnp.mean(xc * xc, axis=-1, keepdims=True)
    y = xc * lax.rsqrt(var + EPS) * g.astype(jnp.float32) + b.astype(jnp.float32)
    return y.astype(x.dtype)


def causal_dwconv(u, w):
    k = w.shape[0]
    return lax.conv_general_dilated(
        u, w[:, None, :].astype(u.dtype), window_strides=(1,), padding=[(k - 1, 0)],
        dimension_numbers=("NWC", "WIO", "NWC"), feature_group_count=u.shape[-1])


def multiscale_pool(u):
    b, s, _ = u.shape
    uf = u.astype(jnp.float32).reshape(b, s, N_POOL_GROUPS, POOL_GROUP_DIM)
    cs = jnp.cumsum(uf, axis=1)
    t = jnp.arange(s)
    outs = []
    for g, w in enumerate(POOL_WINDOWS):
        c = cs[:, :, g]
        lag = jnp.pad(c, ((0, 0), (w, 0), (0, 0)))[:, :s]
        cnt = jnp.minimum(t + 1, w).astype(jnp.float32)[None, :, None]
        outs.append((c - lag) / cnt - uf[:, :, g])
    return jnp.stack(outs, axis=2).astype(u.dtype)


def setup_inputs(seed: int = 0) -> dict:
    key = jax.random.key(seed)
    ks = jax.random.split(key, 24)
    f32 = jnp.float32
    nrm = lambda k, shape, scale: jax.random.normal(k, shape, f32) * scale
    gain = lambda k, shape: 1.0 + 0.05 * jax.random.normal(k, shape, f32)
    return {
        "x": jax.random.normal(ks[0], (BATCH, SEQ, D_MODEL), f32),
        "mem": jax.random.normal(ks[1], (BATCH, MEM_LEN, D_MODEL), f32),
        "mix_norm_g": gain(ks[2], (DEPTH, D_MODEL)),
        "w_in": nrm(ks[3], (DEPTH, D_MODEL, N_IN), D_MODEL ** -0.5),
        "conv_dw_w": nrm(ks[4], (DEPTH, CONV_K, C_CONV), CONV_K ** -0.5),
        "conv_dw_b": nrm(ks[5], (DEPTH, C_CONV), 0.02),
        "conv_ln_g": gain(ks[6], (DEPTH, C_CONV)),
        "conv_ln_b": nrm(ks[7], (DEPTH, C_CONV), 0.02),
        "w_conv_out": nrm(ks[8], (DEPTH, C_CONV, D_MODEL), C_CONV ** -0.5),
        "w_pool_grp": nrm(ks[9], (DEPTH, N_POOL_GROUPS, POOL_GROUP_DIM, POOL_GROUP_OUT), POOL_GROUP_DIM ** -0.5),
        "pool_scale": gain(ks[10], (DEPTH, D_MODEL)),
        "w_out": nrm(ks[11], (DEPTH, D_MODEL, D_MODEL), D_MODEL ** -0.5),
        "xattn_norm_g": gain(ks[12], (DEPTH, D_MODEL)),
        "mem_norm_g": gain(ks[13], (D_MODEL,)),
        "w_q": nrm(ks[14], (DEPTH, D_MODEL, D_MODEL), D_MODEL ** -0.5),
        "w_kv": nrm(ks[15], (DEPTH, D_MODEL, 2 * D_MODEL), D_MODEL ** -0.5),
        "w_o": nrm(ks[16], (DEPTH, D_MODEL, D_MODEL), D_MODEL ** -0.5),
        "ffn_norm_g": gain(ks[17], (DEPTH, D_MODEL)),
        "w_up": nrm(ks[18], (DEPTH, D_MODEL, 2 * D_FF), D_MODEL ** -0.5),
        "ffn_dw_w": nrm(ks[19], (DEPTH, FFN_CONV_K, 2 * D_FF), FFN_CONV_K ** -0.5),
        "w_down": nrm(ks[20], (DEPTH, D_FF, D_MODEL), D_FF ** -0.5),
        "final_norm_g": gain(ks[21], (D_MODEL,)),
    }


def reference(x, mem, mix_norm_g, w_in, conv_dw_w, conv_dw_b, conv_ln_g, conv_ln_b, w_conv_out,
              w_pool_grp, pool_scale, w_out, xattn_norm_g, mem_norm_g, w_q, w_kv, w_o,
              ffn_norm_g, w_up, ffn_dw_w, w_down, final_norm_g):
    b, s, d = x.shape
    m_len = mem.shape[1]
    mem_n = rms_norm(mem, mem_norm_g)
    split_at = [C_CONV, 2 * C_CONV, 2 * C_CONV + C_POOL, 2 * C_CONV + C_POOL + D_MODEL]
    xa_scale = XA_HEAD_DIM ** -0.5
    for l in range(DEPTH):
        h = rms_norm(x, mix_norm_g[l])
        proj = h @ w_in[l]
        a, gl, u_pool, g_conv, g_pool = jnp.split(proj, split_at, axis=-1)
        yc = a * jax.nn.sigmoid(gl)
        yc = causal_dwconv(yc, conv_dw_w[l]) + conv_dw_b[l]
        yc = jax.nn.silu(layer_norm(yc, conv_ln_g[l], conv_ln_b[l]))
        yc = yc @ w_conv_out[l]
        zp = multiscale_pool(u_pool)
        yp = jnp.einsum("bsgc,gcd->bsgd", zp, w_pool_grp[l]).reshape(b, s, d) * pool_scale[l]
        merged = jax.nn.sigmoid(g_conv) * yc + jax.nn.sigmoid(g_pool) * yp
        x = x + merged @ w_out[l]
        hq = rms_norm(x, xattn_norm_g[l])
        q = (hq @ w_q[l]).reshape(b, s, XA_HEADS, XA_HEAD_DIM)
        kv = mem_n @ w_kv[l]
        k, v = jnp.split(kv, 2, axis=-1)
        k = k.reshape(b, m_len, XA_HEADS, XA_HEAD_DIM)
        v = v.reshape(b, m_len, XA_HEADS, XA_HEAD_DIM)
        sc = jnp.einsum("bshd,bmhd->bhsm", q, k).astype(jnp.float32) * xa_scale
        pr = jax.nn.softmax(sc, axis=-1).astype(v.dtype)
        att = jnp.einsum("bhsm,bmhd->bshd", pr, v).reshape(b, s, d)
        x = x + att @ w_o[l]
        hf = rms_norm(x, ffn_norm_g[l])
        up = causal_dwconv(hf @ w_up[l], ffn_dw_w[l])
        gate, val = jnp.split(up, 2, axis=-1)
        x = x + (jax.nn.gelu(gate) * val) @ w_down[l]
    return rms_norm(x, final_norm_g)
```

```python
import contextlib
import numpy as np
import concourse.bass as bass
import concourse.mybir as mybir
from concourse.bass_utils import run_bass_kernel_spmd

F32 = mybir.dt.float32
BF16 = mybir.dt.bfloat16
AF = mybir.ActivationFunctionType
ALU = mybir.AluOpType

ENGS = ["pe", "act", "dve", "pool", "sp"]
D = 1024
SEQ = 2048
T = 1024
ST = 512
MEM = 256
NSLOT = 6
KEYG = 256
EPS = 1e-6
NVL = 300
NV = 616


def block_table():
    blocks = [("wx", 5120), ("agl0", 4096), ("agl1", 4096), ("upb", 4096)]
    blocks += [("m%d" % j, 4096) for j in range(4)]
    blocks += [("wo0", 4096), ("wo1", 4096)]
    blocks += [("kv%d" % j, 4096) for j in range(4)]
    blocks += [("wq0", 4096), ("wq1", 4096), ("woa0", 4096), ("woa1", 4096)]
    for hh in range(2):
        for gi, n in enumerate((4, 4, 3)):
            blocks.append(("g%d_%d" % (hh, gi), 8 * n * 128))
            blocks.append(("v%d_%d" % (hh, gi), 8 * n * 128))
        for oc in range(8):
            blocks.append(("d%d_%d" % (hh, oc), 11 * 128))
    table = {}
    off = 0
    for name, n in blocks:
        table[name] = (off, n)
        off += n
    return blocks, table, off


BLOCKS, BTAB, FL = block_table()


def consumption_order(hf):
    names = ["agl0", "agl1", "upb", "m0", "m1", "m2", "m3", "wo0", "wo1"]
    if hf == 0:
        names += ["kv0", "kv1", "kv2", "kv3"]
    names += ["wq0", "wq1", "woa0", "woa1"]
    for gi in range(3):
        names += ["g0_%d" % gi, "v0_%d" % gi]
    names += ["g1_0", "v1_0"]
    names += ["d0_%d" % oc for oc in range(8)]
    for gi in (1, 2):
        names += ["g1_%d" % gi, "v1_%d" % gi]
    names += ["d1_%d" % oc for oc in range(8)] * 2
    return names


def kblock(W, cols):
    Wc = W[:, cols]
    K, n = Wc.shape
    return np.ascontiguousarray(Wc.reshape(K // 128, 128, n).transpose(1, 0, 2).reshape(128, (K // 128) * n))


def prep_layer_stream(inp, l):
    out = np.empty((128, FL), np.float32)

    def put(name, arr):
        off, n = BTAB[name]
        assert arr.shape == (128, n), (name, arr.shape, n)
        out[:, off:off + n] = arr

    w_in = inp["w_in"][l]
    r = np.arange
    wx = np.concatenate([kblock(inp["w_conv_out"][l], r(1024)),
                         inp["w_pool_grp"][l].transpose(1, 0, 2).reshape(128, 1024)], axis=1)
    put("wx", wx)
    put("agl0", kblock(w_in, r(0, 512)))
    put("agl1", kblock(w_in, r(512, 1024)))
    put("upb", kblock(w_in, r(1024, 1536)))
    for j in range(4):
        cols = np.concatenate([r(1536 + 256 * j, 1536 + 256 * (j + 1)), r(2560 + 256 * j, 2560 + 256 * (j + 1))])
        put("m%d" % j, kblock(w_in, cols))
    for j in range(2):
        put("wo%d" % j, kblock(inp["w_out"][l], r(512 * j, 512 * (j + 1))))
        put("wq%d" % j, kblock(inp["w_q"][l], r(512 * j, 512 * (j + 1))))
        put("woa%d" % j, kblock(inp["w_o"][l], r(512 * j, 512 * (j + 1))))
    for j in range(4):
        put("kv%d" % j, kblock(inp["w_kv"][l], r(512 * j, 512 * (j + 1))))
    w_up = inp["w_up"][l]
    w_dn = inp["w_down"][l]
    for hh in range(2):
        for gi, (c0, n) in enumerate(((0, 4), (4, 4), (8, 3))):
            h0 = (hh * 11 + c0) * 128
            put("g%d_%d" % (hh, gi), kblock(w_up, r(h0, h0 + n * 128)))
            put("v%d_%d" % (hh, gi), kblock(w_up, r(2816 + h0, 2816 + h0 + n * 128)))
        for oc in range(8):
            put("d%d_%d" % (hh, oc), kblock(w_dn[hh * 1408:(hh + 1) * 1408], r(oc * 128, (oc + 1) * 128)))
    return out


def prep_vecs(inp):
    v = np.zeros((128, NV), np.float32)
    for l in range(2):
        b = l * NVL
        v[:, b + 0:b + 8] = inp["mix_norm_g"][l].reshape(8, 128).T
        v[:, b + 8:b + 132] = inp["conv_dw_w"][l].reshape(31, 4, 128).transpose(2, 1, 0).reshape(128, 124)
        v[:, b + 132:b + 136] = inp["conv_dw_b"][l].reshape(4, 128).T
        v[:, b + 136:b + 140] = inp["conv_ln_g"][l].reshape(4, 128).T
        v[:, b + 140:b + 144] = inp["conv_ln_b"][l].reshape(4, 128).T
        v[:, b + 144:b + 152] = inp["pool_scale"][l].reshape(8, 128).T
        v[:, b + 152:b + 160] = inp["xattn_norm_g"][l].reshape(8, 128).T
        v[:, b + 160:b + 168] = inp["ffn_norm_g"][l].reshape(8, 128).T
        v[:, b + 168:b + 300] = inp["ffn_dw_w"][l].reshape(3, 44, 128).transpose(2, 1, 0).reshape(128, 132)
    v[:, 600:608] = inp["mem_norm_g"].reshape(8, 128).T
    v[:, 608:616] = inp["final_norm_g"].reshape(8, 128).T
    return v


class Sched:
    def __init__(self, nc):
        self.nc = nc
        self.ops = {e: [] for e in ENGS}
        self.last_w = {}
        self.readers = {}
        self.dma_sems = {}

    def add(self, eng, fn, reads=(), writes=(), dma_sem=None):
        deps = set()
        for k in reads:
            t = self.last_w.get(k)
            if t is not None:
                deps.add(t)
        for k in writes:
            t = self.last_w.get(k)
            if t is not None:
                deps.add(t)
            rs = self.readers.get(k)
            if rs:
                deps.update(rs.values())
        idx = len(self.ops[eng])
        if dma_sem is not None:
            st = self.dma_sems.setdefault(dma_sem, [None, 0])
            st[1] += 16
            tok = ("d", dma_sem, st[1])
        else:
            tok = ("e", eng, idx)
        if eng == "pe":
            deps = {d for d in deps if not (d[0] == "e" and d[1] == "pe")}
        deps.discard(tok)
        self.ops[eng].append(dict(fn=fn, deps=deps, tok=tok, dma_sem=dma_sem, signal=False, val=None))
        for k in writes:
            self.last_w[k] = tok
            self.readers[k] = {}
        rk = (tok[0], tok[1])
        for k in reads:
            self.readers.setdefault(k, {})[rk] = tok
        return tok

    def emit(self, final_waits=()):
        nc = self.nc
        ops = self.ops
        for e in ENGS:
            for op in ops[e]:
                for d in op["deps"]:
                    if d[0] == "e":
                        ops[d[1]][d[2]]["signal"] = True
        for d in final_waits:
            if d[0] == "e":
                ops[d[1]][d[2]]["signal"] = True
        for e in ENGS:
            c = 0
            for op in ops[e]:
                if op["tok"][0] == "e" and op["signal"]:
                    c += 1
                    op["val"] = c
        with contextlib.ExitStack() as es:
            esem = {e: es.enter_context(nc.semaphore("s_" + e)) for e in ENGS}
            dma_sems = self.dma_sems
            for name, st in dma_sems.items():
                st[0] = es.enter_context(nc.semaphore("d_" + name))
            block = es.enter_context(nc.Block())

            def resolve(d):
                if d[0] == "e":
                    return esem[d[1]], ops[d[1]][d[2]]["val"], ("e", d[1])
                return dma_sems[d[1]][0], d[2], ("d", d[1])

            def run(e, handle, extra=()):
                known = {}

                def do_wait(d):
                    sem, val, key = resolve(d)
                    if known.get(key, 0) >= val:
                        return
                    handle.wait_ge(sem, val)
                    known[key] = val

                for op in ops[e]:
                    best = {}
                    for d in op["deps"]:
                        sem, val, key = resolve(d)
                        if val > best.get(key, (0, None))[0]:
                            best[key] = (val, d)
                    for key in sorted(best, key=str):
                        do_wait(best[key][1])
                    ins = op["fn"](handle)
                    if op["dma_sem"] is not None:
                        ins.then_inc(dma_sems[op["dma_sem"]][0], 16)
                    elif op["signal"]:
                        ins.then_inc(esem[e], 1)
                for d in extra:
                    do_wait(d)

            @block.tensor
            def _(h):
                run("pe", h)

            @block.scalar
            def _(h):
                run("act", h)

            @block.vector
            def _(h):
                run("dve", h)

            @block.gpsimd
            def _(h):
                run("pool", h)

            @block.sync
            def _(h):
                run("sp", h, extra=final_waits)


class View:
    __slots__ = ("ap", "keys")

    def __init__(self, ap, keys):
        self.ap = ap
        self.keys = keys


class Buf:
    def __init__(self, ap, kname, es, boff=0):
        self.ap = ap
        self.kname = kname
        self.es = es
        self.boff = boff
        self.n = ap.shape[1]

    def v(self, lo=0, hi=None):
        if hi is None:
            hi = self.n
        assert 0 <= lo < hi <= self.n, (self.kname, lo, hi, self.n)
        b0 = self.boff + lo * self.es
        b1 = self.boff + hi * self.es
        keys = [(self.kname, k) for k in range(b0 // KEYG, (b1 - 1) // KEYG + 1)]
        return View(self.ap[:, lo:hi], keys)


def build_nc(nseq=2, nhalf=2, layers=(0, 1), final_norm=True, debug=False):
    nc = bass.Bass("TRN2", target_bir_lowering=False)
    TT_ = nhalf * T
    if debug:
        dbg = nc.dram_tensor("dbg", [3, nseq, D, TT_], F32, kind="ExternalOutput").ap()
    xT = nc.dram_tensor("xT", [nseq, D, TT_], F32, kind="ExternalInput").ap()
    memT = nc.dram_tensor("memT", [nseq, D, MEM], F32, kind="ExternalInput").ap()
    wst = nc.dram_tensor("wst", [2, 128, FL], F32, kind="ExternalInput").ap()
    vecs = nc.dram_tensor("vecs", [128, NV], F32, kind="ExternalInput").ap()
    yT = nc.dram_tensor("yT", [nseq, D, TT_], F32, kind="ExternalOutput").ap()

    with contextlib.ExitStack() as es:
        def sb(name, n, dt):
            return es.enter_context(nc.sbuf_tensor(name, [128, n], dt))

        S = Sched(nc)

        Xt = sb("X", 8 * T, F32)
        Ht = sb("H", 8 * T, BF16)
        RINGt = sb("RING", NSLOT * 4096, BF16)
        WXt = sb("WX", 5120, BF16)
        KVt = sb("KV", 8192, BF16)
        MEMNt = sb("MEMN", 8 * MEM, BF16)
        VECt = sb("VEC", NV, F32)
        ONESKt = sb("ONESK", 128, BF16)
        ONES5t = sb("ONES5", 128, BF16)
        ONES5Ft = sb("ONES5F", 128, F32)
        ONES1t = sb("ONES1", 128, BF16)
        IDENTt = sb("IDENT", 128, BF16)
        IDENTFt = sb("IDENTF", 128, F32)
        INVCt = sb("INVC", 16, F32)
        EPSt = sb("EPS", 1, F32)
        Ft = [sb("F%d" % i, 1040, F32) for i in range(4)]
        GBt = [sb("GB%d" % i, T, BF16) for i in range(4)]
        SQt = [sb("SQ%d" % i, ST, BF16) for i in range(4)]
        RSt = sb("RS", T, F32)
        GLHt = sb("GLH", 2 * 4 * 32, BF16)
        UPHt = sb("UPH", 2 * 4 * 16, F32)
        FHt = sb("FH", 2 * 44 * 2, BF16)
        ARB = 44032
        ARt = sb("AR", ARB // 4, F32)
        PSt = [es.enter_context(nc.psum_tensor("ps%d" % i, [128, ST], F32)) for i in range(8)]

        def whole(t, name, es_):
            return Buf(t[:], name, es_)

        X = [Buf(Xt[:, c * T:(c + 1) * T], "X", 4, c * T * 4) for c in range(8)]
        H = [Buf(Ht[:, c * T:(c + 1) * T], "H", 2, c * T * 2) for c in range(8)]
        RING = [Buf(RINGt[:, i * 4096:(i + 1) * 4096], "RING", 2, i * 8192) for i in range(NSLOT)]
        WX = whole(WXt, "WX", 2)
        KV = whole(KVt, "KV", 2)
        MEMN = [Buf(MEMNt[:, c * MEM:(c + 1) * MEM], "MEMN", 2, c * MEM * 2) for c in range(8)]
        VEC = whole(VECt, "VEC", 4)
        ONESK = whole(ONESKt, "ONESK", 2)
        ONES5 = whole(ONES5t, "ONES5", 2)
        ONES5F = whole(ONES5Ft, "ONES5F", 4)
        ONES1 = whole(ONES1t, "ONES1", 2)
        IDENT = whole(IDENTt, "IDENT", 2)
        IDENTF = whole(IDENTFt, "IDENTF", 4)
        INVC = whole(INVCt, "INVC", 4)
        EPSB = whole(EPSt, "EPS", 4)
        F = [whole(Ft[i], "F%d" % i, 4) for i in range(4)]
        GB = [whole(GBt[i], "GB%d" % i, 2) for i in range(4)]
        SQB = [whole(SQt[i], "SQ%d" % i, 2) for i in range(4)]
        RSB = whole(RSt, "RS", 4)
        GLH = whole(GLHt, "GLH", 2)
        UPH = whole(UPHt, "UPH", 4)
        FH = whole(FHt, "FH", 2)
        PS = [whole(PSt[i], "ps%d" % i, 4) for i in range(8)]

        def arena(boff, ncols, dt):
            esz = 4 if dt == F32 else 2
            nb = ncols * esz
            assert boff % 4 == 0 and boff + nb <= ARB
            ap = ARt[:, boff // 4:(boff + nb) // 4]
            if dt != F32:
                ap = ap.bitcast(dt)
            return Buf(ap, "AR", esz, boff)

        A_ = [arena(c * 2048, T, BF16) for c in range(4)]
        GLU = [arena(8192 + c * 2304, 1056, BF16) for c in range(4)]
        ZP = [arena(17408 + c * 2048, T, BF16) for c in range(4)]
        DG = [arena(25600 + i * 1024, 512, BF16) for i in range(2)]
        YCB = [arena(27648 + c * 4096, T, F32) for c in range(4)]
        MG = [arena(27648 + i * 2048, T, BF16) for i in range(8)]
        UP = [arena(8192 + i * 4352, 1040, F32) for i in range(2)]
        Q = [arena(c * 2048, T, BF16) for c in range(8)]
        ATT = [arena(16384 + c * 2048, T, BF16) for c in range(8)]
        EB = [arena(32768 + i * 1024, ST, BF16) for i in range(4)]
        ACTV = [arena(j * 2048, T, BF16) for j in range(15)]
        UB = [arena(30720 + i * 2304, 1056, BF16) for i in range(4)]
        MEMF = [arena(c * 1024, MEM, F32) for c in range(8)]

        def _sc(x):
            return (x.ap, x.keys) if isinstance(x, View) else (x, [])

        def ACT(out, in_, func, bias=None, scale=1.0):
            rd = list(in_.keys)
            kw = {}
            if bias is not None:
                b, k = _sc(bias)
                kw["bias"] = b
                rd += k
            s, k = _sc(scale)
            rd += k
            S.add("act", lambda h: h.activation(out=out.ap, in_=in_.ap, func=func, scale=s, **kw),
                  reads=rd, writes=out.keys)

        def TT(eng, out, a, b, op):
            S.add(eng, lambda h: h.tensor_tensor(out=out.ap, in0=a.ap, in1=b.ap, op=op),
                  reads=a.keys + b.keys, writes=out.keys)

        def TS(eng, out, a, s1, op0):
            s, k = _sc(s1)
            S.add(eng, lambda h: h.tensor_scalar(out=out.ap, in0=a.ap, scalar1=s, scalar2=None, op0=op0),
                  reads=a.keys + k, writes=out.keys)

        def STT(eng, out, a, sc, b, op0, op1):
            s, k = _sc(sc)
            S.add(eng, lambda h: h.scalar_tensor_tensor(out=out.ap, in0=a.ap, scalar=s, in1=b.ap, op0=op0, op1=op1),
                  reads=a.keys + b.keys + k, writes=out.keys)

        def CP(eng, out, in_):
            S.add(eng, lambda h: h.tensor_copy(out=out.ap, in_=in_.ap), reads=in_.keys, writes=out.keys)

        def RECIP(out, in_):
            S.add("dve", lambda h: h.reciprocal(out=out.ap, in_=in_.ap), reads=in_.keys, writes=out.keys)

        def MSET(out, val):
            S.add("pool", lambda h: h.memset(out.ap, val), writes=out.keys)

        def MM(out, lhsT, rhs, start, stop):
            S.add("pe", lambda h: h.matmul(out.ap, lhsT=lhsT.ap, rhs=rhs.ap, start=start, stop=stop),
                  reads=lhsT.keys + rhs.keys, writes=out.keys)

        def vcol(c):
            return VEC.v(c, c + 1)

        def vbc(c):
            v_ = VEC.v(c, c + 1)
            return View(v_.ap.to_broadcast([128, 128]), v_.keys)

        class Rot:
            def __init__(self, items):
                self.items = items
                self.i = 0
                self.held = set()

            def next(self):
                for _ in range(len(self.items)):
                    j = self.i
                    self.i = (self.i + 1) % len(self.items)
                    if j not in self.held:
                        return self.items[j]
                raise RuntimeError("rot exhausted")

            def take(self):
                for _ in range(len(self.items)):
                    j = self.i
                    self.i = (self.i + 1) % len(self.items)
                    if j not in self.held:
                        self.held.add(j)
                        return self.items[j]
                raise RuntimeError("rot exhausted")

            def release_all(self):
                self.held = set()

            def release(self, item):
                self.held.discard(self.items.index(item))

        P = Rot(PS)
        GBR = Rot(GB)
        DGR = Rot(DG)
        EBR = Rot(EB)
        UBR = Rot(UB)
        SQR = Rot(SQB)

        def sub(buf, s, base=0):
            return buf.v(base + s * ST, base + (s + 1) * ST)

        seq_list = []
        for b in range(nseq):
            for hf in range(nhalf):
                for l in layers:
                    seq_list += [(l, n) for n in consumption_order(hf)]

        class Ring:
            cons = 0
            issued = 0

        def ring_get(l, name):
            n = Ring.cons
            assert seq_list[n] == (l, name), (seq_list[n], l, name)
            while Ring.issued < min(len(seq_list), n + NSLOT - 1):
                m = Ring.issued
                slot = m % NSLOT
                ll, nm = seq_list[m]
                off, ne = BTAB[nm]
                dst = RING[slot].v(0, ne).ap
                src = wst[ll, :, off:off + ne]
                S.add("pool", lambda h, d=dst, s_=src: h.dma_start(out=d, in_=s_),
                      writes=RING[slot].v().keys, dma_sem="ring%d" % slot)
                Ring.issued += 1
            Ring.cons += 1
            return RING[n % NSLOT]

        S.add("sp", lambda h: h.dma_start(out=VEC.ap, in_=vecs), writes=VEC.v().keys, dma_sem="vec")
        MSET(ONESK.v(), 1.0 / 1024)
        MSET(ONES5.v(), 1.0 / 512)
        MSET(ONES5F.v(), 1.0 / 512)
        MSET(ONES1.v(), 1.0)
        MSET(EPSB.v(), EPS)
        for t_ in range(16):
            MSET(INVC.v(t_, t_ + 1), 1.0 / (t_ + 1))
        MSET(IDENTF.v(), 0.0)
        S.add("pool", lambda h: h.affine_select(out=IDENTF.ap, in_=IDENTF.ap, pattern=[[-1, 128]],
                                                compare_op=ALU.not_equal, fill=1.0, base=0, channel_multiplier=1),
              reads=IDENTF.v().keys, writes=IDENTF.v().keys)
        CP("pool", IDENT.v(), IDENTF.v())

        def norm_stats(src_chunks, width, ones):
            nsub = (width + ST - 1) // ST
            w_ = min(width, ST)
            pss = [P.take() for _ in range(nsub)]
            for c in range(8):
                sq = GBR.next()
                ACT(sq.v(0, width), src_chunks[c].v(0, width), AF.Square)
                for s in range(nsub):
                    MM(pss[s].v(0, w_), ones.v(), sq.v(s * w_, (s + 1) * w_), c == 0, c == 7)
            for s in range(nsub):
                rstd_from(F[1].v(s * w_, (s + 1) * w_), pss[s].v(0, w_))
                P.release(pss[s])
            return F[1]

        def first_norm_s0(gbase):
            pn0 = P.take()
            for c in range(8):
                ACT(sub(H[c], 0), sub(X[c], 0), AF.Square)
            for c in range(8):
                MM(pn0.v(), ONESK.v(), sub(H[c], 0), c == 0, c == 7)
            rstd_from(RSB.v(0, ST), pn0.v())
            P.release(pn0)
            for c in range(8):
                norm_apply(0, c, "H", gbase)

        def first_norm_s1(gbase):
            pn1 = P.take()
            for c in range(8):
                ACT(sub(H[c], 1), sub(X[c], 1), AF.Square)
            for c in range(8):
                defer(lambda c=c: MM(pn1.v(), ONESK.v(), sub(H[c], 1), c == 0, c == 7), 1 + c // 2)

            def fin():
                rstd_from(RSB.v(ST, 2 * ST), pn1.v())
                P.release(pn1)
            defer(fin, 5)
            for c in range(8):
                defer(lambda c=c: norm_apply(1, c, "H", gbase), 5 + c // 4)

        deferred = []
        tickc = [0]
        dseq = [0]
        cur = {}

        def defer(fn, lag=1, tag=1):
            dseq[0] += 1
            deferred.append((tickc[0] + lag, dseq[0], fn, tag))
            deferred.sort(key=lambda t_: (t_[0], t_[1]))

        def flush_tag(tag):
            keep = []
            while deferred:
                it = deferred.pop(0)
                if it[3] == tag:
                    it[2]()
                else:
                    keep.append(it)
            deferred.extend(keep)

        def tick():
            tickc[0] += 1
            while deferred and deferred[0][0] <= tickc[0]:
                deferred.pop(0)[2]()

        def flush():
            while deferred:
                deferred.pop(0)[2]()

        def rstd_from(out, ms_view):
            ACT(out, ms_view, AF.Ln, bias=EPSB.v())
            ACT(out, out, AF.Exp, scale=-0.5)

        def proj_chunk(blk, ncol, col0, rhs_chunks, evac):
            nk = len(rhs_chunks)
            pss = [P.next(), P.next()]
            for kc in range(nk):
                for s in range(2):
                    MM(pss[s].v(), blk.v(kc * ncol + col0, kc * ncol + col0 + 128), sub(rhs_chunks[kc], s), kc == 0, kc == nk - 1)
            for s in range(2):
                evac(s, pss[s])
            tick()

        def proj_group(items, smajor):
            reads_h = any(it[3] is H for it in items)
            if reads_h:
                flush_tag(0)
            if not smajor:
                if reads_h:
                    flush()
                for blk, ncol, col0, rhs, evac in items:
                    proj_chunk(blk() if callable(blk) else blk, ncol, col0, rhs, evac)
                return
            for s in range(2):
                if s == 1 and reads_h:
                    flush()
                for blk, ncol, col0, rhs, evac in items:
                    b_ = blk() if callable(blk) else blk
                    nk = len(rhs)
                    ps = P.next()
                    for kc in range(nk):
                        MM(ps.v(), b_.v(kc * ncol + col0, kc * ncol + col0 + 128), sub(rhs[kc], s), kc == 0, kc == nk - 1)
                    evac(s, ps)
                    tick()

        def norm_stats_fin(s, pn):
            rs = RSB.v(s * ST, (s + 1) * ST)
            rstd_from(rs, pn[s].v())
            P.release(pn[s])

        def norm_apply(s, c, kind, gbase):
            rs = RSB.v(s * ST, (s + 1) * ST)
            if kind == "H":
                STT("dve", sub(H[c], s), sub(X[c], s), vcol(gbase + c), rs, ALU.mult, ALU.mult)
            else:
                o = F[2 + c % 2]
                STT("dve", o.v(0, ST), sub(X[c], s), vcol(608 + c), rs, ALU.mult, ALU.mult)
                b, hf = cur["b"], cur["hf"]
                tok = S.add("pool", lambda h, b=b, hf=hf, c=c, s=s, s_=o.v(0, ST).ap: h.dma_start(
                    out=yT[b, c * 128:(c + 1) * 128, hf * T + s * ST:hf * T + (s + 1) * ST], in_=s_),
                    reads=o.v(0, ST).keys, dma_sem="out%d" % (c % 2))
                out_tokens.append(tok)

        def residual_stage(items_wo_evac, kind, gbase, apply_rate=1):
            pn = [P.take(), P.take()]

            prev_sq = {}

            def mk(oc):
                def ev(s, ps):
                    TT("dve", sub(X[oc], s), ps.v(), sub(X[oc], s), ALU.add)
                    sq = SQR.next()
                    ACT(sq.v(), sub(X[oc], s), AF.Square)
                    if oc % 2 == 0:
                        prev_sq[s] = sq
                    else:
                        TT("dve", sq.v(), sq.v(), prev_sq[s].v(), ALU.add)
                        defer(lambda: MM(pn[s].v(), ONESK.v(), sq.v(), oc == 1, oc == 7), 3, s)
                    if oc == 7:
                        defer(lambda: norm_stats_fin(s, pn), 4, s)
                        rate = max(apply_rate, 2) if s == 0 else apply_rate
                        for c in range(8):
                            defer(lambda c=c: norm_apply(s, c, kind, gbase), 4 + c // rate, s)
                return ev

            items = [(blk, ncol, col0, rhs, mk(oc)) for oc, (blk, ncol, col0, rhs) in enumerate(items_wo_evac)]
            proj_group(items, True)

        def mixer(l, hf, first):
            LB = l * NVL
            off, ne = BTAB["wx"]
            S.add("pool", lambda h: h.dma_start(out=WX.ap, in_=wst[l, :, off:off + ne]), writes=WX.v().keys, dma_sem="wx")
            for c in range(4):
                CP("pool", GLU[c].v(0, 32), GLH.v((l * 4 + c) * 32, (l * 4 + c + 1) * 32))
            b0 = ring_get(l, "agl0")
            b1 = ring_get(l, "agl1")
            sgs = [GBR.next() for _ in range(4)]

            def ev_a(c):
                return lambda s, ps: ACT(sub(A_[c], s), ps.v(), AF.Copy)

            def ev_gl(c):
                def ev(s, ps):
                    ACT(sub(sgs[c], s), ps.v(), AF.Sigmoid)
                    TT("dve", sub(GLU[c], s, 32), sub(A_[c], s), sub(sgs[c], s), ALU.mult)
                return ev

            items = [(b0, 512, c * 128, H, ev_a(c)) for c in range(4)] + [(b1, 512, c * 128, H, ev_gl(c)) for c in range(4)]
            proj_group(items, True)
            if hf == 0 and nhalf > 1:
                for c in range(4):
                    CP("pool", GLH.v((l * 4 + c) * 32, (l * 4 + c + 1) * 32), GLU[c].v(1024, 1056))
            psmu = [P.take(), P.take()]
            psex = [P.take(), P.take()]
            for c in range(4):
                pss = [P.next(), P.next()]
                for k0 in range(0, 31, 4):
                    nk_ = min(4, 31 - k0)
                    dgb = DGR.next()
                    col = LB + 8 + c * 31 + k0
                    wv = VEC.v(col, col + nk_)
                    o_ap = dgb.v(0, nk_ * 128).ap.rearrange("p (k j) -> p k j", k=nk_)
                    i_ap = IDENT.ap.unsqueeze(1).to_broadcast([128, nk_, 128])
                    w_ap = wv.ap.unsqueeze(2).to_broadcast([128, nk_, 128])
                    S.add("dve", lambda h, o_ap=o_ap, i_ap=i_ap, w_ap=w_ap: h.tensor_tensor(out=o_ap, in0=i_ap, in1=w_ap, op=ALU.mult),
                          reads=IDENT.v().keys + wv.keys, writes=dgb.v(0, nk_ * 128).keys)
                    for kk in range(nk_):
                        k = k0 + kk
                        for s in range(2):
                            MM(pss[s].v(), dgb.v(kk * 128, (kk + 1) * 128), GLU[c].v(s * ST + 2 + k, s * ST + 2 + k + ST), k == 0, k == 30)
                sq = GBR.next()
                yb = GBR.next()
                for s in range(2):
                    ACT(sub(YCB[c], s), pss[s].v(), AF.Identity, bias=vcol(LB + 132 + c))
                    ACT(sub(yb, s), pss[s].v(), AF.Identity, bias=vcol(LB + 132 + c))
                    ACT(sub(sq, s), pss[s].v(), AF.Square, bias=vcol(LB + 132 + c))
                for s in range(2):
                    MM(psmu[s].v(), ONES5.v(), sub(yb, s), c == 0, c == 3)
                    MM(psex[s].v(), ONES5.v(), sub(sq, s), c == 0, c == 3)
            for s in range(2):
                ACT(sub(F[0], s), psmu[s].v(), AF.Copy)
                ACT(sub(F[1], s), psmu[s].v(), AF.Square)
                TT("dve", sub(F[1], s), psex[s].v(), sub(F[1], s), ALU.subtract)
                rstd_from(sub(F[1], s), sub(F[1], s))
            for pb in psmu + psex:
                P.release(pb)
            P.i = (P.items.index(psex[1]) + 1) % len(P.items)
            for c in range(4):
                TT("dve", YCB[c].v(), YCB[c].v(), F[0].v(0, T), ALU.subtract)
                TT("dve", YCB[c].v(), YCB[c].v(), F[1].v(0, T), ALU.mult)
                ACT(A_[c].v(), YCB[c].v(), AF.Silu, bias=vcol(LB + 140 + c), scale=vcol(LB + 136 + c))
            blk = ring_get(l, "upb")
            for g in range(4):
                up = UP[g % 2]
                CP("pool", up.v(0, 16), UPH.v((l * 4 + g) * 16, (l * 4 + g + 1) * 16))
                proj_chunk(blk, 512, g * 128, H, lambda s, ps, up=up: ACT(sub(up, s, 16), ps.v(), AF.Copy))
                if hf == 0 and nhalf > 1:
                    CP("pool", UPH.v((l * 4 + g) * 16, (l * 4 + g + 1) * 16), up.v(1024, 1040))
                w = 2 << g
                cur = up
                bufs = [F[2], F[3]]
                st = 0
                for step in range(g + 1):
                    d = 1 << step
                    st += d
                    dst = bufs[step % 2]
                    TT("dve", dst.v(st, 1040), cur.v(st, 1040), cur.v(st - d, 1040 - d), ALU.add)
                    cur = dst
                STT("dve", ZP[g].v(), cur.v(16, 1040), 1.0 / w, up.v(16, 1040), ALU.mult, ALU.subtract)
                if hf == 0:
                    oth = bufs[(g + 1) % 2]
                    TT("dve", oth.v(0, w - 1), cur.v(16, 16 + w - 1), INVC.v(0, w - 1), ALU.mult)
                    TT("dve", ZP[g].v(0, w - 1), oth.v(0, w - 1), up.v(16, 16 + w - 1), ALU.subtract)
            gates = {}
            mblk = {}

            def do_gates(i):
                j, ii = divmod(i, 2)
                if ii == 0:
                    mblk[j] = ring_get(l, "m%d" % j)
                blk_ = mblk[j]
                gc = GBR.next()
                proj_chunk(blk_, 512, ii * 128, H, lambda s, ps, gc=gc: ACT(sub(gc, s), ps.v(), AF.Sigmoid))
                gp = GBR.next()
                proj_chunk(blk_, 512, 256 + ii * 128, H, lambda s, ps, gp=gp: ACT(sub(gp, s), ps.v(), AF.Sigmoid))
                gates[i] = (gc, gp)

            do_gates(0)
            for i in range(8):
                if i + 1 < 8:
                    do_gates(i + 1)
                gc, gp = gates.pop(i)
                ma = F[i % 2]
                mb = F[2 + i % 2]
                for s in range(2):
                    psc = P.next()
                    for kc in range(4):
                        MM(psc.v(), WX.v(kc * 1024 + i * 128, kc * 1024 + (i + 1) * 128), sub(A_[kc], s), kc == 0, kc == 3)
                    psp = P.next()
                    wcol = 4096 + (i // 2) * 256 + (i % 2) * 128
                    MM(psp.v(), WX.v(wcol, wcol + 128), sub(ZP[i // 2], s), True, True)
                    TT("dve", sub(ma, s), psc.v(), sub(gc, s), ALU.mult)
                    STT("dve", sub(mb, s), psp.v(), vcol(LB + 144 + i), sub(gp, s), ALU.mult, ALU.mult)
                    TT("dve", sub(MG[i], s), sub(ma, s), sub(mb, s), ALU.add)
            wob = [ring_get(l, "wo0"), ring_get(l, "wo1")]
            residual_stage([(wob[oc // 4], 512, (oc % 4) * 128, MG) for oc in range(8)], "H", LB + 152)

        def attn(l, hf):
            LB = l * NVL
            kb = l * 4096
            KT = [Buf(KV.ap[:, kb + c * 256:kb + (c + 1) * 256], "KV", 2, (kb + c * 256) * 2) for c in range(8)]
            VV = [Buf(KV.ap[:, kb + 2048 + m * 1024:kb + 2048 + (m + 1) * 1024], "KV", 2, (kb + 2048 + m * 1024) * 2) for m in range(2)]
            if hf == 0:
                for jb in range(2):
                    blk = ring_get(l, "kv%d" % jb)
                    for cc in range(4):
                        ps = P.next()
                        for kc in range(8):
                            MM(ps.v(0, 256), blk.v(kc * 512 + cc * 128, kc * 512 + (cc + 1) * 128), MEMN[kc].v(), kc == 0, kc == 7)
                        ACT(KT[jb * 4 + cc].v(), ps.v(0, 256), AF.Copy)
                        tick()
                for jb in range(2):
                    blk = ring_get(l, "kv%d" % (2 + jb))
                    for mc in range(2):
                        ps = P.next()
                        for kc in range(8):
                            MM(ps.v(), MEMN[kc].v(mc * 128, (mc + 1) * 128), blk.v(kc * 512, (kc + 1) * 512), kc == 0, kc == 7)
                        ACT(VV[mc].v(jb * 512, (jb + 1) * 512), ps.v(), AF.Copy)
                        tick()
            wqb = [ring_get(l, "wq0"), ring_get(l, "wq1")]
            proj_group([(wqb[qc // 4], 512, (qc % 4) * 128, H,
                         (lambda s, ps, qc=qc: ACT(sub(Q[qc], s), ps.v(), AF.Identity, scale=1.0 / 16))) for qc in range(8)], True)
            ri = 0
            for hd in range(4):
                for s in range(2):
                    E = []
                    for mc in range(2):
                        ps = P.next()
                        for dc in range(2):
                            MM(ps.v(), KT[2 * hd + dc].v(mc * 128, (mc + 1) * 128), sub(Q[2 * hd + dc], s), dc == 0, dc == 1)
                        e = EBR.next()
                        ACT(e.v(), ps.v(), AF.Exp)
                        E.append(e)
                    psd = P.next()
                    for mc in range(2):
                        MM(psd.v(), ONES1.v(), E[mc].v(), mc == 0, mc == 1)
                    rd = F[ri % 4]
                    ri += 1
                    ACT(rd.v(0, ST), psd.v(), AF.Ln)
                    ACT(rd.v(0, ST), rd.v(0, ST), AF.Exp, scale=-1.0)
                    for dc in range(2):
                        pso = P.next()
                        for mc in range(2):
                            MM(pso.v(), VV[mc].v(hd * 256 + dc * 128, hd * 256 + (dc + 1) * 128), E[mc].v(), mc == 0, mc == 1)
                        TT("dve", sub(ATT[2 * hd + dc], s), pso.v(), rd.v(0, ST), ALU.mult)
            wab = [ring_get(l, "woa0"), ring_get(l, "woa1")]
            residual_stage([(wab[oc // 4], 512, (oc % 4) * 128, ATT) for oc in range(8)], "H", LB + 160, apply_rate=4)

        def ffn(l, hf):
            LB = l * NVL
            cnt = [0]
            groups = ((0, 4), (4, 4), (8, 3))

            def slot_of(hh, jl):
                if hh == 0:
                    return jl
                return 11 + jl if jl < 4 else jl - 4

            def do_pairs(gblk, vblk, n, plist, smajor, split=False):
                items = []
                posts = []
                for jj, j, slot in plist:
                    res = []
                    for blk, chunk, is_gate in ((gblk, j, True), (vblk, 22 + j, False)):
                        fb = F[(0 if is_gate else 2) + cnt[0] % 2]
                        ub = UBR.next()
                        fh = FH.v((l * 44 + chunk) * 2, (l * 44 + chunk) * 2 + 2)
                        wb = LB + 168 + chunk * 3
                        ACT(ub.v(0, 2), fh, AF.Copy)
                        if is_gate:
                            def ev(s, ps, ub=ub, fb=fb, wb=wb):
                                ACT(sub(ub, s, 2), ps.v(), AF.Copy)
                                ACT(sub(fb, s), ps.v(), AF.Identity, scale=vcol(wb + 2))
                        else:
                            def ev(s, ps, ub=ub):
                                ACT(sub(ub, s, 2), ps.v(), AF.Copy)
                        items.append((blk, n * 128, jj * 128, H, ev))
                        res.append((fb, ub, fh, wb, is_gate))
                    cnt[0] += 1
                    posts.append((res, slot))
                proj_group(items, smajor)
                segs = [(0, ST), (ST, T)] if split else [(0, T)]
                for res, slot in posts:
                    if hf == 0 and nhalf > 1:
                        for fb, ub, fh, wb, is_gate in res:
                            ACT(fh, ub.v(1024, 1026), AF.Copy)
                    ge = GBR.next()
                    cvb = GBR.next()
                    for a_, b_ in segs:
                        for fb, ub, fh, wb, is_gate in res:
                            if not is_gate:
                                TS("dve", fb.v(a_, b_), ub.v(2 + a_, 2 + b_), vcol(wb + 2), ALU.mult)
                            STT("dve", fb.v(a_, b_), ub.v(1 + a_, 1 + b_), vcol(wb + 1), fb.v(a_, b_), ALU.mult, ALU.add)
                            dst = fb.v(a_, b_) if is_gate else cvb.v(a_, b_)
                            STT("dve", dst, ub.v(a_, b_), vcol(wb + 0), fb.v(a_, b_), ALU.mult, ALU.add)
                        ACT(ge.v(a_, b_), res[0][0].v(a_, b_), AF.Gelu_apprx_tanh)
                        TT("dve", ACTV[slot].v(a_, b_), ge.v(a_, b_), cvb.v(a_, b_), ALU.mult)

            def phase_a(hh, gis, head=False):
                for gi in gis:
                    c0, n = groups[gi]
                    gblk = ring_get(l, "g%d_%d" % (hh, gi))
                    vblk = ring_get(l, "v%d_%d" % (hh, gi))
                    plist = [(jj, hh * 11 + c0 + jj, slot_of(hh, c0 + jj)) for jj in range(n)]
                    if head and gi == 0:
                        do_pairs(gblk, vblk, n, plist[:2], True)
                        plist = plist[2:]
                    for pi_, p_ in enumerate(plist):
                        last = (gi == gis[-1]) and (pi_ == len(plist) - 1) and gi == 2
                        do_pairs(gblk, vblk, n, [p_], last, split=last)

            phase_a(0, (0, 1, 2), head=True)
            phase_a(1, (0,))
            rhs0 = [ACTV[slot_of(0, jl)] for jl in range(11)]
            for oc in range(8):
                blk = ring_get(l, "d0_%d" % oc)
                proj_chunk(blk, 128, 0, rhs0, residual_add(oc))
            phase_a(1, (1, 2))
            rhs1 = [ACTV[slot_of(1, jl)] for jl in range(11)]
            li = layers.index(l)
            if li + 1 < len(layers):
                kind, gbase = "H", layers[li + 1] * NVL
            else:
                kind, gbase = "final", 608
            residual_stage([((lambda oc=oc: ring_get(l, "d1_%d" % oc)), 128, 0, rhs1) for oc in range(8)], kind, gbase)

        def residual_add(oc):
            def ev(s, ps):
                TT("dve", sub(X[oc], s), ps.v(), sub(X[oc], s), ALU.add)
            return ev

        out_tokens = []
        for b in range(nseq):
            MSET(GLH.v(), 0.0)
            MSET(UPH.v(), 0.0)
            MSET(FH.v(), 0.0)
            for hf in range(nhalf):
                x_ap = Xt[:].rearrange("p (c t) -> p c t", c=8)
                x_keys = [k for c in range(8) for k in X[c].v().keys]
                def x_load(s_):
                    for c in range(8):
                        S.add("sp", lambda h, b=b, hf=hf, c=c, s_=s_: h.dma_start(
                            out=sub(X[c], s_).ap, in_=xT[b, c * 128:(c + 1) * 128, hf * T + s_ * ST:hf * T + (s_ + 1) * ST]),
                            writes=sub(X[c], s_).keys, dma_sem="xld%d_%d" % (c, s_))

                x_load(0)
                gb0 = layers[0] * NVL
                first_norm_s0(gb0)
                if hf == 0:
                    memf_ap = ARt[:, 0:8 * MEM].rearrange("p (c m) -> p c m", c=8)
                    memf_keys = [k for c in range(8) for k in MEMF[c].v().keys]
                    S.add("sp", lambda h, b=b, d=memf_ap: h.dma_start(out=d, in_=memT[b].rearrange("(c p) m -> p c m", p=128)),
                          writes=memf_keys, dma_sem="mem")
                    rs = norm_stats(MEMF, MEM, ONESK)
                    for c in range(8):
                        STT("dve", MEMN[c].v(), MEMF[c].v(), vcol(600 + c), rs.v(0, MEM), ALU.mult, ALU.mult)
                flush()
                x_load(1)
                first_norm_s1(gb0)
                cur["b"], cur["hf"] = b, hf
                for li_, l in enumerate(layers):
                    for si in range(3):
                        if si == 0:
                            mixer(l, hf, li_ == 0)
                        elif si == 1:
                            attn(l, hf)
                        else:
                            ffn(l, hf)
                        if debug and l == layers[-1]:
                            S.add("sp", lambda h, b=b, hf=hf, si=si, d=x_ap: h.dma_start(
                                out=dbg[si, b, :, hf * T:(hf + 1) * T].rearrange("(c p) t -> p c t", p=128), in_=d),
                                reads=x_keys, dma_sem="dbg")
        flush()
        assert Ring.cons == len(seq_list)
        build_nc.last_sched = S
        S.emit(final_waits=out_tokens[-2:])
    return nc


_NC_CACHE = {}


def kernel(**inputs):
    inp = {k: np.asarray(v) for k, v in inputs.items()}
    x = inp["x"]
    mem = inp["mem"]
    B = x.shape[0]
    ncores = 8
    per = B // ncores
    wst = np.stack([prep_layer_stream(inp, 0), prep_layer_stream(inp, 1)], axis=0)
    vecs = prep_vecs(inp)
    if "nc" not in _NC_CACHE:
        _NC_CACHE["nc"] = build_nc(per, SEQ // T, (0, 1), True)
    nc = _NC_CACHE["nc"]
    in_maps = []
    for c in range(ncores):
        xs = x[c * per:(c + 1) * per]
        ms = mem[c * per:(c + 1) * per]
        in_maps.append({
            "xT": np.ascontiguousarray(xs.transpose(0, 2, 1)),
            "memT": np.ascontiguousarray(ms.transpose(0, 2, 1)),
            "wst": wst,
            "vecs": vecs,
        })
    res = run_bass_kernel_spmd(nc, in_maps, core_ids=list(range(ncores)))
    outs = [np.asarray(r["yT"]).transpose(0, 2, 1) for r in res.results]
    return np.ascontiguousarray(np.concatenate(outs, axis=0).astype(np.float32))
```

```python
import contextlib
import numpy as np
import concourse.bass as bass
import concourse.mybir as mybir
from concourse.bass_utils import run_bass_kernel_spmd

F32 = mybir.dt.float32
BF16 = mybir.dt.bfloat16
AF = mybir.ActivationFunctionType
ALU = mybir.AluOpType

ENGS = ["pe", "act", "dve", "pool", "sp"]
D = 1024
SEQ = 2048
T = 1024
ST = 512
MEM = 256
NSLOT = 6
KEYG = 256
EPS = 1e-6
NVL = 300
NV = 616


def block_table():
    blocks = [("wx", 5120), ("agl0", 4096), ("agl1", 4096), ("upb", 4096)]
    blocks += [("m%d" % j, 4096) for j in range(4)]
    blocks += [("wo0", 4096), ("wo1", 4096)]
    blocks += [("kv%d" % j, 4096) for j in range(4)]
    blocks += [("wq0", 4096), ("wq1", 4096), ("woa0", 4096), ("woa1", 4096)]
    for hh in range(2):
        for gi, n in enumerate((4, 4, 3)):
            blocks.append(("g%d_%d" % (hh, gi), 8 * n * 128))
            blocks.append(("v%d_%d" % (hh, gi), 8 * n * 128))
        for oc in range(8):
            blocks.append(("d%d_%d" % (hh, oc), 11 * 128))
    table = {}
    off = 0
    for name, n in blocks:
        table[name] = (off, n)
        off += n
    return blocks, table, off


BLOCKS, BTAB, FL = block_table()


def consumption_order(hf):
    names = ["agl0", "agl1", "upb", "m0", "m1", "m2", "m3", "wo0", "wo1"]
    if hf == 0:
        names += ["kv0", "kv1", "kv2", "kv3"]
    names += ["wq0", "wq1", "woa0", "woa1"]
    for gi in range(3):
        names += ["g0_%d" % gi, "v0_%d" % gi]
    names += ["g1_0", "v1_0"]
    names += ["d0_%d" % oc for oc in range(8)]
    for gi in (1, 2):
        names += ["g1_%d" % gi, "v1_%d" % gi]
    names += ["d1_%d" % oc for oc in range(8)] * 2
    return names


def kblock(W, cols):
    Wc = W[:, cols]
    K, n = Wc.shape
    return np.ascontiguousarray(Wc.reshape(K // 128, 128, n).transpose(1, 0, 2).reshape(128, (K // 128) * n))


def prep_layer_stream(inp, l):
    out = np.empty((128, FL), np.float32)

    def put(name, arr):
        off, n = BTAB[name]
        assert arr.shape == (128, n), (name, arr.shape, n)
        out[:, off:off + n] = arr

    w_in = inp["w_in"][l]
    r = np.arange
    wx = np.concatenate([kblock(inp["w_conv_out"][l], r(1024)),
                         inp["w_pool_grp"][l].transpose(1, 0, 2).reshape(128, 1024)], axis=1)
    put("wx", wx)
    put("agl0", kblock(w_in, r(0, 512)))
    put("agl1", kblock(w_in, r(512, 1024)))
    put("upb", kblock(w_in, r(1024, 1536)))
    for j in range(4):
        cols = np.concatenate([r(1536 + 256 * j, 1536 + 256 * (j + 1)), r(2560 + 256 * j, 2560 + 256 * (j + 1))])
        put("m%d" % j, kblock(w_in, cols))
    for j in range(2):
        put("wo%d" % j, kblock(inp["w_out"][l], r(512 * j, 512 * (j + 1))))
        put("wq%d" % j, kblock(inp["w_q"][l], r(512 * j, 512 * (j + 1))))
        put("woa%d" % j, kblock(inp["w_o"][l], r(512 * j, 512 * (j + 1))))
    for j in range(4):
        put("kv%d" % j, kblock(inp["w_kv"][l], r(512 * j, 512 * (j + 1))))
    w_up = inp["w_up"][l]
    w_dn = inp["w_down"][l]
    for hh in range(2):
        for gi, (c0, n) in enumerate(((0, 4), (4, 4), (8, 3))):
            h0 = (hh * 11 + c0) * 128
            put("g%d_%d" % (hh, gi), kblock(w_up, r(h0, h0 + n * 128)))
            put("v%d_%d" % (hh, gi), kblock(w_up, r(2816 + h0, 2816 + h0 + n * 128)))
        for oc in range(8):
            put("d%d_%d" % (hh, oc), kblock(w_dn[hh * 1408:(hh + 1) * 1408], r(oc * 128, (oc + 1) * 128)))
    return out


def prep_vecs(inp):
    v = np.zeros((128, NV), np.float32)
    for l in range(2):
        b = l * NVL
        v[:, b + 0:b + 8] = inp["mix_norm_g"][l].reshape(8, 128).T
        v[:, b + 8:b + 132] = inp["conv_dw_w"][l].reshape(31, 4, 128).transpose(2, 1, 0).reshape(128, 124)
        v[:, b + 132:b + 136] = inp["conv_dw_b"][l].reshape(4, 128).T
        v[:, b + 136:b + 140] = inp["conv_ln_g"][l].reshape(4, 128).T
        v[:, b + 140:b + 144] = inp["conv_ln_b"][l].reshape(4, 128).T
        v[:, b + 144:b + 152] = inp["pool_scale"][l].reshape(8, 128).T
        v[:, b + 152:b + 160] = inp["xattn_norm_g"][l].reshape(8, 128).T
        v[:, b + 160:b + 168] = inp["ffn_norm_g"][l].reshape(8, 128).T
        v[:, b + 168:b + 300] = inp["ffn_dw_w"][l].reshape(3, 44, 128).transpose(2, 1, 0).reshape(128, 132)
    v[:, 600:608] = inp["mem_norm_g"].reshape(8, 128).T
    v[:, 608:616] = inp["final_norm_g"].reshape(8, 128).T
    return v


class Sched:
    def __init__(self, nc):
        self.nc = nc
        self.ops = {e: [] for e in ENGS}
        self.last_w = {}
        self.readers = {}
        self.dma_sems = {}

    def add(self, eng, fn, reads=(), writes=(), dma_sem=None):
        deps = set()
        for k in reads:
            t = self.last_w.get(k)
            if t is not None:
                deps.add(t)
        for k in writes:
            t = self.last_w.get(k)
            if t is not None:
                deps.add(t)
            rs = self.readers.get(k)
            if rs:
                deps.update(rs.values())
        idx = len(self.ops[eng])
        if dma_sem is not None:
            st = self.dma_sems.setdefault(dma_sem, [None, 0])
            st[1] += 16
            tok = ("d", dma_sem, st[1])
        else:
            tok = ("e", eng, idx)
        if eng == "pe":
            deps = {d for d in deps if not (d[0] == "e" and d[1] == "pe")}
        deps.discard(tok)
        self.ops[eng].append(dict(fn=fn, deps=deps, tok=tok, dma_sem=dma_sem, signal=False, val=None))
        for k in writes:
            self.last_w[k] = tok
            self.readers[k] = {}
        rk = (tok[0], tok[1])
        for k in reads:
            self.readers.setdefault(k, {})[rk] = tok
        return tok

    def emit(self, final_waits=()):
        nc = self.nc
        ops = self.ops
        for e in ENGS:
            for op in ops[e]:
                for d in op["deps"]:
                    if d[0] == "e":
                        ops[d[1]][d[2]]["signal"] = True
        for d in final_waits:
            if d[0] == "e":
                ops[d[1]][d[2]]["signal"] = True
        for e in ENGS:
            c = 0
            for op in ops[e]:
                if op["tok"][0] == "e" and op["signal"]:
                    c += 1
                    op["val"] = c
        with contextlib.ExitStack() as es:
            esem = {e: es.enter_context(nc.semaphore("s_" + e)) for e in ENGS}
            dma_sems = self.dma_sems
            for name, st in dma_sems.items():
                st[0] = es.enter_context(nc.semaphore("d_" + name))
            block = es.enter_context(nc.Block())

            def resolve(d):
                if d[0] == "e":
                    return esem[d[1]], ops[d[1]][d[2]]["val"], ("e", d[1])
                return dma_sems[d[1]][0], d[2], ("d", d[1])

            def run(e, handle, extra=()):
                known = {}

                def do_wait(d):
                    sem, val, key = resolve(d)
                    if known.get(key, 0) >= val:
                        return
                    handle.wait_ge(sem, val)
                    known[key] = val

                for op in ops[e]:
                    best = {}
                    for d in op["deps"]:
                        sem, val, key = resolve(d)
                        if val > best.get(key, (0, None))[0]:
                            best[key] = (val, d)
                    for key in sorted(best, key=str):
                        do_wait(best[key][1])
                    ins = op["fn"](handle)
                    if op["dma_sem"] is not None:
                        ins.then_inc(dma_sems[op["dma_sem"]][0], 16)
                    elif op["signal"]:
                        ins.then_inc(esem[e], 1)
                for d in extra:
                    do_wait(d)

            @block.tensor
            def _(h):
                run("pe", h)

            @block.scalar
            def _(h):
                run("act", h)

            @block.vector
            def _(h):
                run("dve", h)

            @block.gpsimd
            def _(h):
                run("pool", h)

            @block.sync
            def _(h):
                run("sp", h, extra=final_waits)


class View:
    __slots__ = ("ap", "keys")

    def __init__(self, ap, keys):
        self.ap = ap
        self.keys = keys


class Buf:
    def __init__(self, ap, kname, es, boff=0):
        self.ap = ap
        self.kname = kname
        self.es = es
        self.boff = boff
        self.n = ap.shape[1]

    def v(self, lo=0, hi=None):
        if hi is None:
            hi = self.n
        assert 0 <= lo < hi <= self.n, (self.kname, lo, hi, self.n)
        b0 = self.boff + lo * self.es
        b1 = self.boff + hi * self.es
        keys = [(self.kname, k) for k in range(b0 // KEYG, (b1 - 1) // KEYG + 1)]
        return View(self.ap[:, lo:hi], keys)


def build_nc(nseq=2, nhalf=2, layers=(0, 1), final_norm=True, debug=False):
    nc = bass.Bass("TRN2", target_bir_lowering=False)
    TT_ = nhalf * T
    if debug:
        dbg = nc.dram_tensor("dbg", [3, nseq, D, TT_], F32, kind="ExternalOutput").ap()
    xT = nc.dram_tensor("xT", [nseq, D, TT_], F32, kind="ExternalInput").ap()
    memT = nc.dram_tensor("memT", [nseq, D, MEM], F32, kind="ExternalInput").ap()
    wst = nc.dram_tensor("wst", [2, 128, FL], F32, kind="ExternalInput").ap()
    vecs = nc.dram_tensor("vecs", [128, NV], F32, kind="ExternalInput").ap()
    yT = nc.dram_tensor("yT", [nseq, D, TT_], F32, kind="ExternalOutput").ap()

    with contextlib.ExitStack() as es:
        def sb(name, n, dt):
            return es.enter_context(nc.sbuf_tensor(name, [128, n], dt))

        S = Sched(nc)

        Xt = sb("X", 8 * T, F32)
        Ht = sb("H", 8 * T, BF16)
        RINGt = sb("RING", NSLOT * 4096, BF16)
        WXt = sb("WX", 5120, BF16)
        KVt = sb("KV", 8192, BF16)
        MEMNt = sb("MEMN", 8 * MEM, BF16)
        VECt = sb("VEC", NV, F32)
        ONESKt = sb("ONESK", 128, BF16)
        ONES5t = sb("ONES5", 128, BF16)
        ONES5Ft = sb("ONES5F", 128, F32)
        ONES1t = sb("ONES1", 128, BF16)
        IDENTt = sb("IDENT", 128, BF16)
        IDENTFt = sb("IDENTF", 128, F32)
        INVCt = sb("INVC", 16, F32)
        EPSt = sb("EPS", 1, F32)
        Ft = [sb("F%d" % i, 1040, F32) for i in range(4)]
        GBt = [sb("GB%d" % i, T, BF16) for i in range(4)]
        SQt = [sb("SQ%d" % i, ST, BF16) for i in range(4)]
        RSt = sb("RS", T, F32)
        GLHt = sb("GLH", 2 * 4 * 32, BF16)
        UPHt = sb("UPH", 2 * 4 * 16, F32)
        FHt = sb("FH", 2 * 44 * 2, BF16)
        ARB = 44032
        ARt = sb("AR", ARB // 4, F32)
        PSt = [es.enter_context(nc.psum_tensor("ps%d" % i, [128, ST], F32)) for i in range(8)]

        def whole(t, name, es_):
            return Buf(t[:], name, es_)

        X = [Buf(Xt[:, c * T:(c + 1) * T], "X", 4, c * T * 4) for c in range(8)]
        H = [Buf(Ht[:, c * T:(c + 1) * T], "H", 2, c * T * 2) for c in range(8)]
        RING = [Buf(RINGt[:, i * 4096:(i + 1) * 4096], "RING", 2, i * 8192) for i in range(NSLOT)]
        WX = whole(WXt, "WX", 2)
        KV = whole(KVt, "KV", 2)
        MEMN = [Buf(MEMNt[:, c * MEM:(c + 1) * MEM], "MEMN", 2, c * MEM * 2) for c in range(8)]
        VEC = whole(VECt, "VEC", 4)
        ONESK = whole(ONESKt, "ONESK", 2)
        ONES5 = whole(ONES5t, "ONES5", 2)
        ONES5F = whole(ONES5Ft, "ONES5F", 4)
        ONES1 = whole(ONES1t, "ONES1", 2)
        IDENT = whole(IDENTt, "IDENT", 2)
        IDENTF = whole(IDENTFt, "IDENTF", 4)
        INVC = whole(INVCt, "INVC", 4)
        EPSB = whole(EPSt, "EPS", 4)
        F = [whole(Ft[i], "F%d" % i, 4) for i in range(4)]
        GB = [whole(GBt[i], "GB%d" % i, 2) for i in range(4)]
        SQB = [whole(SQt[i], "SQ%d" % i, 2) for i in range(4)]
        RSB = whole(RSt, "RS", 4)
        GLH = whole(GLHt, "GLH", 2)
        UPH = whole(UPHt, "UPH", 4)
        FH = whole(FHt, "FH", 2)
        PS = [whole(PSt[i], "ps%d" % i, 4) for i in range(8)]

        def arena(boff, ncols, dt):
            esz = 4 if dt == F32 else 2
            nb = ncols * esz
            assert boff % 4 == 0 and boff + nb <= ARB
            ap = ARt[:, boff // 4:(boff + nb) // 4]
            if dt != F32:
                ap = ap.bitcast(dt)
            return Buf(ap, "AR", esz, boff)

        A_ = [arena(c * 2048, T, BF16) for c in range(4)]
        GLU = [arena(8192 + c * 2304, 1056, BF16) for c in range(4)]
        ZP = [arena(17408 + c * 2048, T, BF16) for c in range(4)]
        DG = [arena(25600 + i * 1024, 512, BF16) for i in range(2)]
        YCB = [arena(27648 + c * 4096, T, F32) for c in range(4)]
        MG = [arena(27648 + i * 2048, T, BF16) for i in range(8)]
        UP = [arena(8192 + i * 4352, 1040, F32) for i in range(2)]
        Q = [arena(c * 2048, T, BF16) for c in range(8)]
        ATT = [arena(16384 + c * 2048, T, BF16) for c in range(8)]
        EB = [arena(32768 + i * 1024, ST, BF16) for i in range(4)]
        ACTV = [arena(j * 2048, T, BF16) for j in range(15)]
        UB = [arena(30720 + i * 2304, 1056, BF16) for i in range(4)]
        MEMF = [Buf(Ft[2 + c // 4][:, (c % 4) * MEM:(c % 4 + 1) * MEM], "F%d" % (2 + c // 4), 4, (c % 4) * MEM * 4) for c in range(8)]

        def _sc(x):
            return (x.ap, x.keys) if isinstance(x, View) else (x, [])

        def ACT(out, in_, func, bias=None, scale=1.0):
            rd = list(in_.keys)
            kw = {}
            if bias is not None:
                b, k = _sc(bias)
                kw["bias"] = b
                rd += k
            s, k = _sc(scale)
            rd += k
            S.add("act", lambda h: h.activation(out=out.ap, in_=in_.ap, func=func, scale=s, **kw),
                  reads=rd, writes=out.keys)

        def TT(eng, out, a, b, op):
            S.add(eng, lambda h: h.tensor_tensor(out=out.ap, in0=a.ap, in1=b.ap, op=op),
                  reads=a.keys + b.keys, writes=out.keys)

        def TS(eng, out, a, s1, op0):
            s, k = _sc(s1)
            S.add(eng, lambda h: h.tensor_scalar(out=out.ap, in0=a.ap, scalar1=s, scalar2=None, op0=op0),
                  reads=a.keys + k, writes=out.keys)

        def STT(eng, out, a, sc, b, op0, op1):
            s, k = _sc(sc)
            S.add(eng, lambda h: h.scalar_tensor_tensor(out=out.ap, in0=a.ap, scalar=s, in1=b.ap, op0=op0, op1=op1),
                  reads=a.keys + b.keys + k, writes=out.keys)

        def CP(eng, out, in_):
            S.add(eng, lambda h: h.tensor_copy(out=out.ap, in_=in_.ap), reads=in_.keys, writes=out.keys)

        def RECIP(out, in_):
            S.add("dve", lambda h: h.reciprocal(out=out.ap, in_=in_.ap), reads=in_.keys, writes=out.keys)

        def MSET(out, val):
            S.add("pool", lambda h: h.memset(out.ap, val), writes=out.keys)

        def MM(out, lhsT, rhs, start, stop):
            S.add("pe", lambda h: h.matmul(out.ap, lhsT=lhsT.ap, rhs=rhs.ap, start=start, stop=stop),
                  reads=lhsT.keys + rhs.keys, writes=out.keys)

        def vcol(c):
            return VEC.v(c, c + 1)

        def vbc(c):
            v_ = VEC.v(c, c + 1)
            return View(v_.ap.to_broadcast([128, 128]), v_.keys)

        class Rot:
            def __init__(self, items):
                self.items = items
                self.i = 0
                self.held = set()

            def next(self):
                for _ in range(len(self.items)):
                    j = self.i
                    self.i = (self.i + 1) % len(self.items)
                    if j not in self.held:
                        return self.items[j]
                raise RuntimeError("rot exhausted")

            def take(self):
                for _ in range(len(self.items)):
                    j = self.i
                    self.i = (self.i + 1) % len(self.items)
                    if j not in self.held:
                        self.held.add(j)
                        return self.items[j]
                raise RuntimeError("rot exhausted")

            def release_all(self):
                self.held = set()

            def release(self, item):
                self.held.discard(self.items.index(item))

        P = Rot(PS)
        GBR = Rot(GB)
        DGR = Rot(DG)
        EBR = Rot(EB)
        UBR = Rot(UB)
        SQR = Rot(SQB)

        def sub(buf, s, base=0):
            return buf.v(base + s * ST, base + (s + 1) * ST)

        seq_list = []
        for b in range(nseq):
            for hf in range(nhalf):
                for l in layers:
                    seq_list += [(l, n) for n in consumption_order(hf)]

        class Ring:
            cons = 0
            issued = 0

        def ring_get(l, name):
            n = Ring.cons
            assert seq_list[n] == (l, name), (seq_list[n], l, name)
            while Ring.issued < min(len(seq_list), n + NSLOT - 1):
                m = Ring.issued
                slot = m % NSLOT
                ll, nm = seq_list[m]
                off, ne = BTAB[nm]
                dst = RING[slot].v(0, ne).ap
                src = wst[ll, :, off:off + ne]
                S.add("pool", lambda h, d=dst, s_=src: h.dma_start(out=d, in_=s_),
                      writes=RING[slot].v().keys, dma_sem="ring%d" % slot)
                Ring.issued += 1
            Ring.cons += 1
            return RING[n % NSLOT]

        S.add("sp", lambda h: h.dma_start(out=VEC.ap, in_=vecs), writes=VEC.v().keys, dma_sem="vec")
        MSET(ONESK.v(), 1.0 / 1024)
        MSET(ONES5.v(), 1.0 / 512)
        MSET(ONES5F.v(), 1.0 / 512)
        MSET(ONES1.v(), 1.0)
        MSET(EPSB.v(), EPS)
        for t_ in range(16):
            MSET(INVC.v(t_, t_ + 1), 1.0 / (t_ + 1))
        MSET(IDENTF.v(), 0.0)
        S.add("pool", lambda h: h.affine_select(out=IDENTF.ap, in_=IDENTF.ap, pattern=[[-1, 128]],
                                                compare_op=ALU.not_equal, fill=1.0, base=0, channel_multiplier=1),
              reads=IDENTF.v().keys, writes=IDENTF.v().keys)
        CP("pool", IDENT.v(), IDENTF.v())

        def norm_stats(src_chunks, width, ones):
            nsub = (width + ST - 1) // ST
            w_ = min(width, ST)
            pss = [P.take() for _ in range(nsub)]
            for c in range(8):
                sq = GBR.next()
                ACT(sq.v(0, width), src_chunks[c].v(0, width), AF.Square)
                for s in range(nsub):
                    MM(pss[s].v(0, w_), ones.v(), sq.v(s * w_, (s + 1) * w_), c == 0, c == 7)
            for s in range(nsub):
                rstd_from(F[1].v(s * w_, (s + 1) * w_), pss[s].v(0, w_))
                P.release(pss[s])
            return F[1]

        def first_norm_s0(gbase):
            pn0 = P.take()
            for c in range(8):
                ACT(sub(H[c], 0), sub(X[c], 0), AF.Square)
            for c in range(8):
                MM(pn0.v(), ONESK.v(), sub(H[c], 0), c == 0, c == 7)
            rstd_from(RSB.v(0, ST), pn0.v())
            P.release(pn0)
            for c in range(8):
                norm_apply(0, c, "H", gbase)

        def first_norm_s1(gbase):
            pn1 = P.take()
            for c in range(8):
                ACT(sub(H[c], 1), sub(X[c], 1), AF.Square)
            for c in range(8):
                defer(lambda c=c: MM(pn1.v(), ONESK.v(), sub(H[c], 1), c == 0, c == 7), 1 + c // 2)

            def fin():
                rstd_from(RSB.v(ST, 2 * ST), pn1.v())
                P.release(pn1)
            defer(fin, 5)
            for c in range(8):
                defer(lambda c=c: norm_apply(1, c, "H", gbase), 5 + c // 4)

        deferred = []
        tickc = [0]
        dseq = [0]
        cur = {}
        pstate = {}

        def defer(fn, lag=1, tag=1):
            dseq[0] += 1
            deferred.append((tickc[0] + lag, dseq[0], fn, tag))
            deferred.sort(key=lambda t_: (t_[0], t_[1]))

        def flush_tag(tag):
            keep = []
            while deferred:
                it = deferred.pop(0)
                if it[3] == tag:
                    it[2]()
                else:
                    keep.append(it)
            deferred.extend(keep)

        def tick():
            tickc[0] += 1
            while deferred and deferred[0][0] <= tickc[0]:
                deferred.pop(0)[2]()

        def flush():
            while deferred:
                deferred.pop(0)[2]()

        def rstd_from(out, ms_view):
            ACT(out, ms_view, AF.Ln, bias=EPSB.v())
            ACT(out, out, AF.Exp, scale=-0.5)

        def proj_chunk(blk, ncol, col0, rhs_chunks, evac):
            nk = len(rhs_chunks)
            pss = [P.next(), P.next()]
            for kc in range(nk):
                for s in range(2):
                    MM(pss[s].v(), blk.v(kc * ncol + col0, kc * ncol + col0 + 128), sub(rhs_chunks[kc], s), kc == 0, kc == nk - 1)
            for s in range(2):
                evac(s, pss[s])
            tick()

        def proj_group(items, smajor):
            reads_h = any(it[3] is H for it in items)
            if reads_h:
                flush_tag(0)
            if not smajor:
                if reads_h:
                    flush()
                for blk, ncol, col0, rhs, evac in items:
                    proj_chunk(blk() if callable(blk) else blk, ncol, col0, rhs, evac)
                return
            for s in range(2):
                if s == 1 and reads_h:
                    flush()
                for blk, ncol, col0, rhs, evac in items:
                    b_ = blk() if callable(blk) else blk
                    nk = len(rhs)
                    ps = P.next()
                    for kc in range(nk):
                        MM(ps.v(), b_.v(kc * ncol + col0, kc * ncol + col0 + 128), sub(rhs[kc], s), kc == 0, kc == nk - 1)
                    evac(s, ps)
                    tick()

        def norm_stats_fin(s, pn):
            rs = RSB.v(s * ST, (s + 1) * ST)
            rstd_from(rs, pn[s].v())
            P.release(pn[s])

        def norm_apply(s, c, kind, gbase):
            rs = RSB.v(s * ST, (s + 1) * ST)
            if kind == "H":
                STT("dve", sub(H[c], s), sub(X[c], s), vcol(gbase + c), rs, ALU.mult, ALU.mult)
            else:
                o = F[c % 4]
                STT("dve", o.v(0, ST), sub(X[c], s), vcol(608 + c), rs, ALU.mult, ALU.mult)
                b, hf = cur["b"], cur["hf"]
                tok = S.add("pool", lambda h, b=b, hf=hf, c=c, s=s, s_=o.v(0, ST).ap: h.dma_start(
                    out=yT[b, c * 128:(c + 1) * 128, hf * T + s * ST:hf * T + (s + 1) * ST], in_=s_),
                    reads=o.v(0, ST).keys, dma_sem="out%d" % (c % 4))
                out_tokens.append(tok)

        def residual_stage(items_wo_evac, kind, gbase, apply_rate=1):
            pn = [P.take(), P.take()]

            prev_sq = {}

            def mk(oc):
                def ev(s, ps):
                    TT("dve", sub(X[oc], s), ps.v(), sub(X[oc], s), ALU.add)
                    sq = SQR.next()
                    ACT(sq.v(), sub(X[oc], s), AF.Square)
                    if oc % 2 == 0:
                        prev_sq[s] = sq
                    else:
                        TT("dve", sq.v(), sq.v(), prev_sq[s].v(), ALU.add)
                        defer(lambda: MM(pn[s].v(), ONESK.v(), sq.v(), oc == 1, oc == 7), 3, s)
                    if oc == 7:
                        defer(lambda: norm_stats_fin(s, pn), 4, s)
                        rate = max(apply_rate, 2) if s == 0 else apply_rate
                        for c in range(8):
                            defer(lambda c=c: norm_apply(s, c, kind, gbase), 4 + c // rate, s)
                return ev

            items = [(blk, ncol, col0, rhs, mk(oc)) for oc, (blk, ncol, col0, rhs) in enumerate(items_wo_evac)]
            proj_group(items, True)

        def mem_squares():
            pstate["pm"] = P.take()
            for c in range(8):
                ACT(MEMN[c].v(), MEMF[c].v(), AF.Square)

        def mem_finish():
            pm = pstate.pop("pm")
            for c in range(8):
                MM(pm.v(0, MEM), ONESK.v(), MEMN[c].v(), c == 0, c == 7)
            rstd_from(F[1].v(0, MEM), pm.v(0, MEM))
            P.release(pm)
            for c in range(8):
                STT("dve", MEMN[c].v(), MEMF[c].v(), vcol(600 + c), F[1].v(0, MEM), ALU.mult, ALU.mult)

        def mixer(l, hf, first):
            LB = l * NVL
            off, ne = BTAB["wx"]
            S.add("pool", lambda h: h.dma_start(out=WX.ap, in_=wst[l, :, off:off + ne]), writes=WX.v().keys, dma_sem="wx")
            for c in range(4):
                CP("pool", GLU[c].v(0, 32), GLH.v((l * 4 + c) * 32, (l * 4 + c + 1) * 32))
            b0 = ring_get(l, "agl0")
            b1 = ring_get(l, "agl1")
            sgs = [GBR.next() for _ in range(4)]

            def ev_a(c):
                return lambda s, ps: ACT(sub(A_[c], s), ps.v(), AF.Copy)

            def ev_gl(c):
                def ev(s, ps):
                    ACT(sub(sgs[c], s), ps.v(), AF.Sigmoid)
                    TT("dve", sub(GLU[c], s, 32), sub(A_[c], s), sub(sgs[c], s), ALU.mult)
                return ev

            items = [(b0, 512, c * 128, H, ev_a(c)) for c in range(4)] + [(b1, 512, c * 128, H, ev_gl(c)) for c in range(4)]
            proj_group(items, True)
            if hf == 0 and nhalf > 1:
                for c in range(4):
                    CP("pool", GLH.v((l * 4 + c) * 32, (l * 4 + c + 1) * 32), GLU[c].v(1024, 1056))
            if pstate.get("mem_pending"):
                pstate["mem_pending"] = False
                mem_squares()
            psmu = [P.take(), P.take()]
            psex = [P.take(), P.take()]
            pending_stats = None
            for c in range(4):
                if c == 1 and "pm" in pstate:
                    mem_finish()
                pss = [P.next(), P.next()]
                for k0 in range(0, 31, 4):
                    if pending_stats is not None and k0 == 12:
                        pending_stats()
                        pending_stats = None
                    nk_ = min(4, 31 - k0)
                    dgb = DGR.next()
                    col = LB + 8 + c * 31 + k0
                    wv = VEC.v(col, col + nk_)
                    o_ap = dgb.v(0, nk_ * 128).ap.rearrange("p (k j) -> p k j", k=nk_)
                    i_ap = IDENT.ap.unsqueeze(1).to_broadcast([128, nk_, 128])
                    w_ap = wv.ap.unsqueeze(2).to_broadcast([128, nk_, 128])
                    S.add("dve", lambda h, o_ap=o_ap, i_ap=i_ap, w_ap=w_ap: h.tensor_tensor(out=o_ap, in0=i_ap, in1=w_ap, op=ALU.mult),
                          reads=IDENT.v().keys + wv.keys, writes=dgb.v(0, nk_ * 128).keys)
                    for kk in range(nk_):
                        k = k0 + kk
                        for s in range(2):
                            MM(pss[s].v(), dgb.v(kk * 128, (kk + 1) * 128), GLU[c].v(s * ST + 2 + k, s * ST + 2 + k + ST), k == 0, k == 30)
                sq = GBR.next()
                yb = GBR.next()
                for s in range(2):
                    ACT(sub(YCB[c], s), pss[s].v(), AF.Identity, bias=vcol(LB + 132 + c))
                    ACT(sub(yb, s), pss[s].v(), AF.Identity, bias=vcol(LB + 132 + c))
                    ACT(sub(sq, s), pss[s].v(), AF.Square, bias=vcol(LB + 132 + c))
                def stats_mm(c=c, yb=yb, sq=sq):
                    for s in range(2):
                        MM(psmu[s].v(), ONES5.v(), sub(yb, s), c == 0, c == 3)
                        MM(psex[s].v(), ONES5.v(), sub(sq, s), c == 0, c == 3)
                pending_stats = stats_mm
            pending_stats()
            for s in range(2):
                ACT(sub(F[0], s), psmu[s].v(), AF.Copy)
                ACT(sub(F[1], s), psmu[s].v(), AF.Square)
                TT("dve", sub(F[1], s), psex[s].v(), sub(F[1], s), ALU.subtract)
                rstd_from(sub(F[1], s), sub(F[1], s))
            for pb in psmu + psex:
                P.release(pb)
            P.i = (P.items.index(psex[1]) + 1) % len(P.items)
            for c in range(4):
                TT("dve", YCB[c].v(), YCB[c].v(), F[0].v(0, T), ALU.subtract)
                TT("dve", YCB[c].v(), YCB[c].v(), F[1].v(0, T), ALU.mult)
                ACT(A_[c].v(), YCB[c].v(), AF.Silu, bias=vcol(LB + 140 + c), scale=vcol(LB + 136 + c))
            blk = ring_get(l, "upb")
            for g in range(4):
                up = UP[g % 2]
                CP("pool", up.v(0, 16), UPH.v((l * 4 + g) * 16, (l * 4 + g + 1) * 16))
                proj_chunk(blk, 512, g * 128, H, lambda s, ps, up=up: ACT(sub(up, s, 16), ps.v(), AF.Copy))
                if hf == 0 and nhalf > 1:
                    CP("pool", UPH.v((l * 4 + g) * 16, (l * 4 + g + 1) * 16), up.v(1024, 1040))
                w = 2 << g
                cur = up
                bufs = [F[2], F[3]]
                st = 0
                for step in range(g + 1):
                    d = 1 << step
                    st += d
                    dst = bufs[step % 2]
                    TT("dve", dst.v(st, 1040), cur.v(st, 1040), cur.v(st - d, 1040 - d), ALU.add)
                    cur = dst
                STT("dve", ZP[g].v(), cur.v(16, 1040), 1.0 / w, up.v(16, 1040), ALU.mult, ALU.subtract)
                if hf == 0:
                    oth = bufs[(g + 1) % 2]
                    TT("dve", oth.v(0, w - 1), cur.v(16, 16 + w - 1), INVC.v(0, w - 1), ALU.mult)
                    TT("dve", ZP[g].v(0, w - 1), oth.v(0, w - 1), up.v(16, 16 + w - 1), ALU.subtract)
            gates = {}
            mblk = {}

            def do_gates(i):
                j, ii = divmod(i, 2)
                if ii == 0:
                    mblk[j] = ring_get(l, "m%d" % j)
                blk_ = mblk[j]
                gc = GBR.next()
                proj_chunk(blk_, 512, ii * 128, H, lambda s, ps, gc=gc: ACT(sub(gc, s), ps.v(), AF.Sigmoid))
                gp = GBR.next()
                proj_chunk(blk_, 512, 256 + ii * 128, H, lambda s, ps, gp=gp: ACT(sub(gp, s), ps.v(), AF.Sigmoid))
                gates[i] = (gc, gp)

            do_gates(0)
            for i in range(8):
                if i + 1 < 8:
                    do_gates(i + 1)
                gc, gp = gates.pop(i)
                ma = F[i % 2]
                mb = F[2 + i % 2]
                for s in range(2):
                    psc = P.next()
                    for kc in range(4):
                        MM(psc.v(), WX.v(kc * 1024 + i * 128, kc * 1024 + (i + 1) * 128), sub(A_[kc], s), kc == 0, kc == 3)
                    psp = P.next()
                    wcol = 4096 + (i // 2) * 256 + (i % 2) * 128
                    MM(psp.v(), WX.v(wcol, wcol + 128), sub(ZP[i // 2], s), True, True)
                    TT("dve", sub(ma, s), psc.v(), sub(gc, s), ALU.mult)
                    STT("dve", sub(mb, s), psp.v(), vcol(LB + 144 + i), sub(gp, s), ALU.mult, ALU.mult)
                    TT("dve", sub(MG[i], s), sub(ma, s), sub(mb, s), ALU.add)
            wob = [ring_get(l, "wo0"), ring_get(l, "wo1")]
            residual_stage([(wob[oc // 4], 512, (oc % 4) * 128, MG) for oc in range(8)], "H", LB + 152)

        def attn(l, hf):
            LB = l * NVL
            kb = l * 4096
            KT = [Buf(KV.ap[:, kb + c * 256:kb + (c + 1) * 256], "KV", 2, (kb + c * 256) * 2) for c in range(8)]
            VV = [Buf(KV.ap[:, kb + 2048 + m * 1024:kb + 2048 + (m + 1) * 1024], "KV", 2, (kb + 2048 + m * 1024) * 2) for m in range(2)]
            if hf == 0:
                for jb in range(2):
                    blk = ring_get(l, "kv%d" % jb)
                    for cc in range(4):
                        ps = P.next()
                        for kc in range(8):
                            MM(ps.v(0, 256), blk.v(kc * 512 + cc * 128, kc * 512 + (cc + 1) * 128), MEMN[kc].v(), kc == 0, kc == 7)
                        ACT(KT[jb * 4 + cc].v(), ps.v(0, 256), AF.Copy)
                        tick()
                for jb in range(2):
                    blk = ring_get(l, "kv%d" % (2 + jb))
                    for mc in range(2):
                        ps = P.next()
                        for kc in range(8):
                            MM(ps.v(), MEMN[kc].v(mc * 128, (mc + 1) * 128), blk.v(kc * 512, (kc + 1) * 512), kc == 0, kc == 7)
                        ACT(VV[mc].v(jb * 512, (jb + 1) * 512), ps.v(), AF.Copy)
                        tick()
            wqb = [ring_get(l, "wq0"), ring_get(l, "wq1")]
            proj_group([(wqb[qc // 4], 512, (qc % 4) * 128, H,
                         (lambda s, ps, qc=qc: ACT(sub(Q[qc], s), ps.v(), AF.Identity, scale=1.0 / 16))) for qc in range(8)], True)
            ri = 0
            for hd in range(4):
                for s in range(2):
                    E = []
                    for mc in range(2):
                        ps = P.next()
                        for dc in range(2):
                            MM(ps.v(), KT[2 * hd + dc].v(mc * 128, (mc + 1) * 128), sub(Q[2 * hd + dc], s), dc == 0, dc == 1)
                        e = EBR.next()
                        ACT(e.v(), ps.v(), AF.Exp)
                        E.append(e)
                    psd = P.next()
                    for mc in range(2):
                        MM(psd.v(), ONES1.v(), E[mc].v(), mc == 0, mc == 1)
                    rd = F[ri % 4]
                    ri += 1
                    ACT(rd.v(0, ST), psd.v(), AF.Ln)
                    ACT(rd.v(0, ST), rd.v(0, ST), AF.Exp, scale=-1.0)
                    for dc in range(2):
                        pso = P.next()
                        for mc in range(2):
                            MM(pso.v(), VV[mc].v(hd * 256 + dc * 128, hd * 256 + (dc + 1) * 128), E[mc].v(), mc == 0, mc == 1)
                        TT("dve", sub(ATT[2 * hd + dc], s), pso.v(), rd.v(0, ST), ALU.mult)
            wab = [ring_get(l, "woa0"), ring_get(l, "woa1")]
            residual_stage([(wab[oc // 4], 512, (oc % 4) * 128, ATT) for oc in range(8)], "H", LB + 160, apply_rate=4)

        def ffn(l, hf):
            LB = l * NVL
            cnt = [0]
            groups = ((0, 4), (4, 4), (8, 3))

            def slot_of(hh, jl):
                if hh == 0:
                    return jl
                return 11 + jl if jl < 4 else jl - 4

            def do_pairs(gblk, vblk, n, plist, smajor, split=False):
                items = []
                posts = []
                for jj, j, slot in plist:
                    res = []
                    for blk, chunk, is_gate in ((gblk, j, True), (vblk, 22 + j, False)):
                        fb = F[(0 if is_gate else 2) + cnt[0] % 2]
                        ub = UBR.next()
                        fh = FH.v((l * 44 + chunk) * 2, (l * 44 + chunk) * 2 + 2)
                        wb = LB + 168 + chunk * 3
                        ACT(ub.v(0, 2), fh, AF.Copy)
                        if is_gate:
                            def ev(s, ps, ub=ub, fb=fb, wb=wb):
                                ACT(sub(ub, s, 2), ps.v(), AF.Copy)
                                ACT(sub(fb, s), ps.v(), AF.Identity, scale=vcol(wb + 2))
                        else:
                            def ev(s, ps, ub=ub):
                                ACT(sub(ub, s, 2), ps.v(), AF.Copy)
                        items.append((blk, n * 128, jj * 128, H, ev))
                        res.append((fb, ub, fh, wb, is_gate))
                    cnt[0] += 1
                    posts.append((res, slot))
                proj_group(items, smajor)
                segs = [(0, ST), (ST, T)] if split else [(0, T)]
                for res, slot in posts:
                    if hf == 0 and nhalf > 1:
                        for fb, ub, fh, wb, is_gate in res:
                            ACT(fh, ub.v(1024, 1026), AF.Copy)
                    ge = GBR.next()
                    cvb = GBR.next()
                    for a_, b_ in segs:
                        for fb, ub, fh, wb, is_gate in res:
                            if not is_gate:
                                TS("dve", fb.v(a_, b_), ub.v(2 + a_, 2 + b_), vcol(wb + 2), ALU.mult)
                            STT("dve", fb.v(a_, b_), ub.v(1 + a_, 1 + b_), vcol(wb + 1), fb.v(a_, b_), ALU.mult, ALU.add)
                            dst = fb.v(a_, b_) if is_gate else cvb.v(a_, b_)
                            STT("dve", dst, ub.v(a_, b_), vcol(wb + 0), fb.v(a_, b_), ALU.mult, ALU.add)
                        ACT(ge.v(a_, b_), res[0][0].v(a_, b_), AF.Gelu_apprx_tanh)
                        TT("dve", ACTV[slot].v(a_, b_), ge.v(a_, b_), cvb.v(a_, b_), ALU.mult)

            def phase_a(hh, gis, head=False):
                for gi in gis:
                    c0, n = groups[gi]
                    gblk = ring_get(l, "g%d_%d" % (hh, gi))
                    vblk = ring_get(l, "v%d_%d" % (hh, gi))
                    plist = [(jj, hh * 11 + c0 + jj, slot_of(hh, c0 + jj)) for jj in range(n)]
                    if head and gi == 0:
                        do_pairs(gblk, vblk, n, plist[:2], True)
                        plist = plist[2:]
                    for pi_, p_ in enumerate(plist):
                        last = (gi == gis[-1]) and (pi_ == len(plist) - 1) and gi == 2
                        do_pairs(gblk, vblk, n, [p_], last, split=last)

            phase_a(0, (0, 1, 2), head=True)
            phase_a(1, (0,))
            rhs0 = [ACTV[slot_of(0, jl)] for jl in range(11)]
            for oc in range(8):
                blk = ring_get(l, "d0_%d" % oc)
                proj_chunk(blk, 128, 0, rhs0, residual_add(oc))
            phase_a(1, (1, 2))
            rhs1 = [ACTV[slot_of(1, jl)] for jl in range(11)]
            li = layers.index(l)
            if li + 1 < len(layers):
                kind, gbase = "H", layers[li + 1] * NVL
            else:
                kind, gbase = "final", 608
            residual_stage([((lambda oc=oc: ring_get(l, "d1_%d" % oc)), 128, 0, rhs1) for oc in range(8)], kind, gbase)

        def residual_add(oc):
            def ev(s, ps):
                TT("dve", sub(X[oc], s), ps.v(), sub(X[oc], s), ALU.add)
            return ev

        out_tokens = []
        for b in range(nseq):
            MSET(GLH.v(), 0.0)
            MSET(UPH.v(), 0.0)
            MSET(FH.v(), 0.0)
            for hf in range(nhalf):
                x_ap = Xt[:].rearrange("p (c t) -> p c t", c=8)
                x_keys = [k for c in range(8) for k in X[c].v().keys]
                def x_load(s_):
                    for c in range(8):
                        S.add("sp", lambda h, b=b, hf=hf, c=c, s_=s_: h.dma_start(
                            out=sub(X[c], s_).ap, in_=xT[b, c * 128:(c + 1) * 128, hf * T + s_ * ST:hf * T + (s_ + 1) * ST]),
                            writes=sub(X[c], s_).keys, dma_sem="xld%d_%d" % (c, s_))

                x_load(0)
                gb0 = layers[0] * NVL
                first_norm_s0(gb0)
                flush()
                x_load(1)
                first_norm_s1(gb0)
                if hf == 0:
                    for i_ in range(2):
                        d_ap = Ft[2 + i_][:, 0:4 * MEM].rearrange("p (c m) -> p c m", c=4)
                        keys_ = [k for c in range(4) for k in MEMF[4 * i_ + c].v().keys]
                        S.add("sp", lambda h, b=b, i_=i_, d=d_ap: h.dma_start(
                            out=d, in_=memT[b, i_ * 512:(i_ + 1) * 512, :].rearrange("(c p) m -> p c m", p=128)),
                            writes=keys_, dma_sem="mem%d" % i_)
                    pstate["mem_pending"] = True
                cur["b"], cur["hf"] = b, hf
                for li_, l in enumerate(layers):
                    for si in range(3):
                        if si == 0:
                            mixer(l, hf, li_ == 0)
                        elif si == 1:
                            attn(l, hf)
                        else:
                            ffn(l, hf)
                        if debug and l == layers[-1]:
                            S.add("sp", lambda h, b=b, hf=hf, si=si, d=x_ap: h.dma_start(
                                out=dbg[si, b, :, hf * T:(hf + 1) * T].rearrange("(c p) t -> p c t", p=128), in_=d),
                                reads=x_keys, dma_sem="dbg")
        flush()
        assert Ring.cons == len(seq_list)
        build_nc.last_sched = S
        S.emit(final_waits=out_tokens[-4:])
    return nc


_NC_CACHE = {}


def kernel(**inputs):
    inp = {k: np.asarray(v) for k, v in inputs.items()}
    x = inp["x"]
    mem = inp["mem"]
    B = x.shape[0]
    ncores = 8
    per = B // ncores
    wst = np.stack([prep_layer_stream(inp, 0), prep_layer_stream(inp, 1)], axis=0)
    vecs = prep_vecs(inp)
    if "nc" not in _NC_CACHE:
        _NC_CACHE["nc"] = build_nc(per, SEQ // T, (0, 1), True)
    nc = _NC_CACHE["nc"]
    in_maps = []
    for c in range(ncores):
        xs = x[c * per:(c + 1) * per]
        ms = mem[c * per:(c + 1) * per]
        in_maps.append({
            "xT": np.ascontiguousarray(xs.transpose(0, 2, 1)),
            "memT": np.ascontiguousarray(ms.transpose(0, 2, 1)),
            "wst": wst,
            "vecs": vecs,
        })
    res = run_bass_kernel_spmd(nc, in_maps, core_ids=list(range(ncores)))
    outs = [np.asarray(r["yT"]).transpose(0, 2, 1) for r in res.results]
    return np.ascontiguousarray(np.concatenate(outs, axis=0).astype(np.float32))
```
